# Optimizing a Trainium2 kernel written in Bass

```python
import jax, jax.numpy as jnp
from jax import lax
import numpy as np

D_MODEL = 4096
BATCH = 1
SEQ = 16384
DEPTH = 2
DEC_BATCH = 16
DEC_SEQ = 16
PAST_LEN = 4096

CHUNK = 64
N_MIXERS = 2
N_CONV_LAYERS = (DEPTH + 1) // 2
N_SB_LAYERS = DEPTH // 2
CONV_W = 3
N_HEADS = 32
HEAD_DIM = D_MODEL // N_HEADS
SB_SCALE = HEAD_DIM ** -0.5
Q_BLOCK = 128
K_BLOCK = 128
N_Q_GROUPS = 8
D_FF = ((8 * D_MODEL + 3 * 256 - 1) // (3 * 256)) * 256
EPS = 1e-6

kernel_name = "streaming_conv_stickbreaking_hybrid_step"


def rms_norm(x, g):
    xf = x.astype(jnp.float32)
    y = xf * lax.rsqrt(jnp.mean(xf * xf, axis=-1, keepdims=True) + EPS)
    return (y * g.astype(jnp.float32)).astype(x.dtype)


def swiglu(h, w_gate, w_up, w_down):
    return (jax.nn.silu(h @ w_gate) * (h @ w_up)) @ w_down


def short_conv_mixer(h, conv_state, w_in, w_conv, w_out):
    b_gate, c_gate, xin = jnp.split(h @ w_in, 3, axis=-1)
    u = c_gate * xin
    u_ext = jnp.concatenate([conv_state.astype(u.dtype), u], axis=1)
    t = h.shape[1]
    conv = u_ext[:, 0:t] * w_conv[0]
    for k in range(1, CONV_W):
        conv = conv + u_ext[:, k:k + t] * w_conv[k]
    y = (b_gate * conv) @ w_out
    new_state = u_ext[:, u_ext.shape[1] - (CONV_W - 1):]
    return y, new_state


def sb_project(h, w_qkv, g_q, g_k):
    b, t, _ = h.shape
    q, k, v = jnp.split(h @ w_qkv, 3, axis=-1)
    q = rms_norm(q.reshape(b, t, N_HEADS, HEAD_DIM), g_q)
    k = rms_norm(k.reshape(b, t, N_HEADS, HEAD_DIM), g_k)
    v = v.reshape(b, t, N_HEADS, HEAD_DIM)
    return q, k, v


def stick_breaking_block(q, k, v, q_pos, k_pos):
    b, tq, _, _ = q.shape
    tk = k.shape[1]
    nkb = tk // K_BLOCK
    z = jnp.einsum("bqhd,bkhd->bhqk", q.astype(jnp.float32), k.astype(jnp.float32)) * SB_SCALE
    causal = k_pos[None, :] < q_pos[:, None]
    log_stay = jnp.where(causal, -jax.nn.softplus(z), 0.0)
    ls = log_stay.reshape(b, N_HEADS, tq, nkb, K_BLOCK)
    idx = jnp.arange(K_BLOCK)
    upper = (idx[:, None] >= idx[None, :]).astype(jnp.float32)
    within = jnp.einsum("bhqnj,js->bhqns", ls, upper)
    blk = jnp.arange(nkb)
    later = (blk[:, None] > blk[None, :]).astype(jnp.float32)
    after = jnp.einsum("bhqm,mn->bhqn", ls.sum(axis=-1), later)
    suffix = (within + after[..., None]).reshape(b, N_HEADS, tq, tk)
    a = jnp.where(causal, jnp.exp(z + suffix), 0.0)
    o = jnp.einsum("bhqk,bkhd->bqhd", a, v.astype(jnp.float32))
    return o.astype(v.dtype)


def sb_prompt(q, k, v):
    b, s, _, _ = q.shape
    nb = s // Q_BLOCK
    n_groups = min(N_Q_GROUPS, nb)
    bounds = [(g * nb) // n_groups for g in range(n_groups + 1)]
    outs = []
    for g in range(n_groups):
        b0, b1 = bounds[g], bounds[g + 1]
        key_len = b1 * Q_BLOCK
        kg, vg = k[:, :key_len], v[:, :key_len]
        k_pos = jnp.arange(key_len, dtype=jnp.int32)
        qg = q[:, b0 * Q_BLOCK:b1 * Q_BLOCK].reshape(b, b1 - b0, Q_BLOCK, N_HEADS, HEAD_DIM).transpose(1, 0, 2, 3, 4)

        def body(args, kg=kg, vg=vg, k_pos=k_pos):
            qi, bi = args
            q_pos = bi * Q_BLOCK + jnp.arange(Q_BLOCK, dtype=jnp.int32)
            return stick_breaking_block(qi, kg, vg, q_pos, k_pos)

        o = lax.map(body, (qg, jnp.arange(b0, b1, dtype=jnp.int32)))
        outs.append(o.transpose(1, 0, 2, 3, 4).reshape(b, (b1 - b0) * Q_BLOCK, N_HEADS, HEAD_DIM))
    return jnp.concatenate(outs, axis=1)


def sb_sample(q, k, v, cache_k, cache_v):
    past = cache_k.shape[1]
    t = q.shape[1]
    total = past + t
    pad = (-total) % K_BLOCK
    bq = q.shape[0]
    zeros = jnp.zeros((bq, pad, N_HEADS, HEAD_DIM), k.dtype)
    k_all = jnp.concatenate([cache_k.astype(k.dtype), k, zeros], axis=1)
    v_all = jnp.concatenate([cache_v.astype(v.dtype), v, zeros], axis=1)
    q_pos = past + jnp.arange(t, dtype=jnp.int32)
    k_pos = jnp.arange(total + pad, dtype=jnp.int32)
    return stick_breaking_block(q, k_all, v_all, q_pos, k_pos)


def setup_inputs(seed: int = 0) -> dict:
    key = jax.random.key(seed)
    ks = jax.random.split(key, 17)
    f32 = jnp.float32

    def w(k, shape, fan_in):
        return jax.random.normal(k, shape, f32) * (fan_in ** -0.5)

    def gain(k, shape):
        return 1.0 + 0.02 * jax.random.normal(k, shape, f32)

    return {
        "x_prompt": jax.random.normal(ks[0], (BATCH, SEQ, D_MODEL), f32),
        "x_sample": jax.random.normal(ks[1], (DEC_BATCH, DEC_SEQ, D_MODEL), f32),
        "state_conv": jax.random.normal(ks[2], (N_CONV_LAYERS, DEC_BATCH, CONV_W - 1, D_MODEL), f32),
        "cache_k": jax.random.normal(ks[3], (N_SB_LAYERS, DEC_BATCH, PAST_LEN, N_HEADS, HEAD_DIM), f32),
        "cache_v": jax.random.normal(ks[4], (N_SB_LAYERS, DEC_BATCH, PAST_LEN, N_HEADS, HEAD_DIM), f32),
        "g_mix": gain(ks[5], (DEPTH, D_MODEL)),
        "g_ffn": gain(ks[6], (DEPTH, D_MODEL)),
        "w_conv_in": w(ks[7], (N_CONV_LAYERS, D_MODEL, 3 * D_MODEL), D_MODEL),
        "w_conv": w(ks[8], (N_CONV_LAYERS, CONV_W, D_MODEL), CONV_W),
        "w_conv_out": w(ks[9], (N_CONV_LAYERS, D_MODEL, D_MODEL), D_MODEL),
        "w_qkv": w(ks[10], (N_SB_LAYERS, D_MODEL, 3 * N_HEADS * HEAD_DIM), D_MODEL),
        "g_q": gain(ks[11], (N_SB_LAYERS, HEAD_DIM)),
        "g_k": gain(ks[12], (N_SB_LAYERS, HEAD_DIM)),
        "w_o": w(ks[13], (N_SB_LAYERS, N_HEADS * HEAD_DIM, D_MODEL), N_HEADS * HEAD_DIM),
        "w_gate": w(ks[14], (DEPTH, D_MODEL, D_FF), D_MODEL),
        "w_up": w(ks[15], (DEPTH, D_MODEL, D_FF), D_MODEL),
        "w_down": w(ks[16], (DEPTH, D_FF, D_MODEL), D_FF),
    }


def reference(x_prompt, x_sample, state_conv, cache_k, cache_v, g_mix, g_ffn,
              w_conv_in, w_conv, w_conv_out, w_qkv, g_q, g_k, w_o,
              w_gate, w_up, w_down):
    xp, xs = x_prompt, x_sample
    conv_p, conv_s, k_p, v_p, k_s, v_s = [], [], [], [], [], []
    for i in range(DEPTH):
        j = i // N_MIXERS
        hp = rms_norm(xp, g_mix[i])
        hs = rms_norm(xs, g_mix[i])
        if i % N_MIXERS == 0:
            fresh = jnp.zeros((xp.shape[0], CONV_W - 1, D_MODEL), xp.dtype)
            yp, sp = short_conv_mixer(hp, fresh, w_conv_in[j], w_conv[j], w_conv_out[j])
            ys, ss = short_conv_mixer(hs, state_conv[j], w_conv_in[j], w_conv[j], w_conv_out[j])
            conv_p.append(sp)
            conv_s.append(ss)
        else:
            qp, kp, vp = sb_project(hp, w_qkv[j], g_q[j], g_k[j])
            qs, kss, vss = sb_project(hs, w_qkv[j], g_q[j], g_k[j])
            op = sb_prompt(qp, kp, vp)
            os_ = sb_sample(qs, kss, vss, cache_k[j], cache_v[j])
            yp = op.reshape(op.shape[0], op.shape[1], N_HEADS * HEAD_DIM) @ w_o[j]
            ys = os_.reshape(os_.shape[0], os_.shape[1], N_HEADS * HEAD_DIM) @ w_o[j]
            k_p.append(kp)
            v_p.append(vp)
            k_s.append(kss)
            v_s.append(vss)
        xp = xp + yp
        xs = xs + ys
        xp = xp + swiglu(rms_norm(xp, g_ffn[i]), w_gate[i], w_up[i], w_down[i])
        xs = xs + swiglu(rms_norm(xs, g_ffn[i]), w_gate[i], w_up[i], w_down[i])
    new_conv_prompt = jnp.stack(conv_p)
    new_conv_sample = jnp.stack(conv_s)
    new_k_prompt = jnp.stack(k_p)
    new_v_prompt = jnp.stack(v_p)
    new_k_sample = jnp.stack(k_s)
    new_v_sample = jnp.stack(v_s)
    return (xp, xs, new_conv_prompt, new_conv_sample, new_k_prompt, new_v_prompt, new_k_sample, new_v_sample)
```

```python
import os
import numpy as np
from contextlib import ExitStack
import concourse.bass as bass
import concourse.mybir as mybir
from concourse.bass_utils import run_bass_kernel_spmd

F32 = mybir.dt.float32
BF16 = mybir.dt.bfloat16
AF = mybir.ActivationFunctionType
ALU = mybir.AluOpType

D = 4096
NH = 32
HD = 128
DFF = int(os.environ.get("K_DFF", "11008"))
KC = D // 128
NCORES = 8
TP = 512
NJ = int(os.environ.get("K_NJ", "4"))
TS = 128
NS = 32
PAST = int(os.environ.get("K_PAST", "4096"))
SEQ = NCORES * NJ * TP
STOP = os.environ.get("K_STOP", "")
EPS = 1e-6
SB_SCALE = HD ** -0.5
GRAN = 256

WNAMES = ["w_in", "w_cout", "w_qkv", "w_o", "w_g0", "w_u0", "w_d0", "w_g1", "w_u1", "w_d1"]
WSHAPES = {"w_in": (D, 3 * D), "w_cout": (D, D), "w_qkv": (D, 3 * D), "w_o": (D, D),
           "w_g0": (D, DFF), "w_u0": (D, DFF), "w_d0": (DFF, D),
           "w_g1": (D, DFF), "w_u1": (D, DFF), "w_d1": (DFF, D)}


class V:
    __slots__ = ("ap", "keys")

    def __init__(self, ap, keys):
        self.ap = ap
        self.keys = keys


class Prog:
    ENG = ("pe", "act", "dve", "pool", "sp")

    def __init__(self, nc, es):
        self.nc = nc
        self.es = es
        self.ops = {e: [] for e in self.ENG}
        self.cnt = {e: 0 for e in ("pe", "act", "dve", "pool")}
        self.psem = {e: es.enter_context(nc.semaphore("pg_" + e)) for e in ("pe", "act", "dve", "pool")}
        self.waited = {e: {} for e in self.ENG}
        self.state = {}
        self.dsem = {}
        self.dval = {}
        self.semobj = {}
        self.nsem = 4
        self.final = {}
        self.dry = False

    def _st(self, k):
        s = self.state.get(k)
        if s is None:
            s = self.state[k] = [[], []]
        return s

    def _deps(self, R, W):
        deps = []
        for v in R:
            for k in v.keys:
                deps.extend(self._st(k)[0])
        for v in W:
            for k in v.keys:
                s = self._st(k)
                deps.extend(s[0])
                deps.extend(s[1])
        return deps

    def _commit(self, R, W, tok, partial=False):
        for v in W:
            for k in v.keys:
                s = self._st(k)
                if partial:
                    best = {}
                    for t in s[0] + [tok]:
                        if best.get(t[0], 0) < t[1]:
                            best[t[0]] = t[1]
                    s[0] = list(best.items())
                else:
                    s[0] = [tok]
                    s[1] = []
        for v in R:
            for k in v.keys:
                s = self._st(k)
                if len(s[1]) > 6:
                    best = {}
                    for t in s[1]:
                        if best.get(t[0], 0) < t[1]:
                            best[t[0]] = t[1]
                    s[1] = list(best.items())
                s[1].append(tok)

    def _waits(self, eng, deps):
        wd = self.waited[eng]
        need = {}
        for (sid, val) in deps:
            if eng == "pe" and sid == "pe":
                continue
            if wd.get(sid, 0) < val and need.get(sid, 0) < val:
                need[sid] = val
        out = []
        for sid, val in need.items():
            wd[sid] = val
            out.append((sid, val))
        return out

    def op(self, eng, fn, R=(), W=()):
        if self.dry:
            return
        deps = self._deps(R, W)
        waits = self._waits(eng, deps)
        self.cnt[eng] += 1
        tok = (eng, self.cnt[eng])
        self.ops[eng].append((waits, fn, (eng, 1)))
        self._commit(R, W, tok)

    def _dma_sem(self, key):
        sid = self.dsem.get(key)
        if sid is None:
            sid = "d%d" % len(self.dsem)
            self.dsem[key] = sid
            self.dval[sid] = 0
            self.semobj[sid] = self.es.enter_context(self.nc.semaphore(sid))
        return sid

    def dma(self, q, out, in_, semkey=None, partial=False):
        if self.dry:
            return
        if partial:
            deps = self._deps([in_], [])
            for k in out.keys:
                deps.extend(self._st(k)[1])
        else:
            deps = self._deps([in_], [out])
        waits = self._waits(q, deps)
        if semkey is None:
            k0 = out.keys[0]
            if k0[0] == "sb":
                semkey = ("dst", k0)
            else:
                k1 = in_.keys[0]
                semkey = ("src", k1) if k1[0] == "sb" else ("dst", k0)
        sid = self._dma_sem(semkey)
        self.dval[sid] += 16
        tok = (sid, self.dval[sid])
        oap, iap = out.ap, in_.ap
        self.ops[q].append((waits, (lambda e: e.dma_start(out=oap, in_=iap)), (sid, 16)))
        self._commit([in_], [out], tok, partial)
        self.final[sid] = self.dval[sid]

    def coll_issue(self, groups, in_, out):
        if self.dry:
            return
        if not hasattr(self, "cpend"):
            self.cpend = []
            self.cidx = 0
            self.csems = []
            for i in range(8):
                sid = "c%d" % i
                self.semobj[sid] = self.es.enter_context(self.nc.semaphore(sid))
                self.csems.append(sid)
        while len(self.cpend) >= 8:
            self.coll_done()
        deps = self._deps([in_], [out])
        waits = self._waits("pool", deps)
        sid = self.csems[self.cidx % 8]
        self.cidx += 1
        iap, oap = in_.ap, out.ap

        def fn(e):
            return e.collective_compute("AllGather", ALU.bypass, replica_groups=groups, ins=[iap], outs=[oap])
        self.ops["pool"].append((waits, fn, (sid, None)))
        self.cpend.append((sid, in_, out))

    def coll_done(self):
        sid, in_, out = self.cpend.pop(0)
        self.cnt["pool"] += 1
        tok = ("pool", self.cnt["pool"])
        so = self.semobj[sid]
        pp = self.psem["pool"]

        scr = self.scratch

        def fn(e):
            e.sem_clear(so)
            return e.memset(scr, 0.0)
        self.ops["pool"].append(([(sid, 1)], fn, ("pool", 1)))
        self._commit([in_], [out], tok)

    def coll_drain(self):
        if self.dry:
            return
        while getattr(self, "cpend", []):
            self.coll_done()

    def sem_of(self, sid):
        if sid in self.psem:
            return self.psem[sid]
        return self.semobj[sid]

    def emit(self):
        nc = self.nc
        fw = [(sid, v) for sid, v in self.final.items()]
        fw += [(e, self.cnt[e]) for e in ("pe", "act", "dve") if self.cnt[e] > 0]
        self.ops["pool"].append((fw, None, None))

        def replay(eng, name):
            for waits, fn, inc in self.ops[name]:
                for sid, val in waits:
                    eng.wait_ge(self.sem_of(sid), val)
                if fn is None:
                    continue
                ins = fn(eng)
                if inc is not None and ins is not None:
                    if inc[1] is None:
                        ins.then_inc(self.sem_of(inc[0]))
                    else:
                        ins.then_inc(self.sem_of(inc[0]), inc[1])

        with nc.Block() as block:
            @block.tensor
            def _(e):
                replay(e, "pe")

            @block.scalar
            def _(e):
                replay(e, "act")

            @block.vector
            def _(e):
                replay(e, "dve")

            @block.gpsimd
            def _(e):
                replay(e, "pool")

            @block.sync
            def _(e):
                replay(e, "sp")


class Arena:
    def __init__(self, nc, es, words):
        self.nc = nc
        self.base = (nc.sbuf_base + 31) // 32 * 32
        self.slab = es.enter_context(nc.sbuf_tensor("arena", [128, words], F32))
        assert nc.sbuf_base == self.base + 4 * words, (nc.sbuf_base, self.base, words)
        self.words = words
        self.top = 0
        self.cache = {}

    def alloc(self, words, align=GRAN):
        self.top = (self.top + align - 1) // align * align
        off = self.top
        self.top += words
        assert self.top <= self.words, (self.top, self.words)
        return off

    @staticmethod
    def keys(w0, w1):
        return [("sb", g) for g in range(w0 // GRAN, (w1 - 1) // GRAN + 1)]

    def _t(self, dt, boff, n):
        k = (dt == F32, boff, n)
        t = self.cache.get(k)
        if t is None:
            t = self.nc.alloc_sbuf_tensor_at("v%d" % len(self.cache), [128, n], dt, offset=self.base + boff,
                                             align_bytes=4 if dt == F32 else 2)
            self.cache[k] = t
        return t

    def f32(self, off, n, p0=0, p1=128):
        return V(self._t(F32, 4 * off, n)[p0:p1, :], self.keys(off, off + n))

    def b16(self, off, boff, n, p0=0, p1=128):
        return V(self._t(BF16, 4 * off + 2 * boff, n)[p0:p1, :],
                 self.keys(off + boff // 2, off + (boff + n + 1) // 2))


def sub(v, ap):
    return V(ap, v.keys)


def build_program():
    nc = bass.Bass("TRN2", target_bir_lowering=False)
    es = ExitStack()
    P = Prog(nc, es)

    def dram_in(name, shape, dt=F32):
        return nc.dram_tensor(name, list(shape), dt, kind="ExternalInput")

    def dram_out(name, shape):
        return nc.dram_tensor(name, list(shape), F32, kind="ExternalOutput")

    xs_d = dram_in("xs", (TS, D))
    xp_d = dram_in("xp", (NJ * TP, D))
    sc_d = dram_in("sc", (4, D))
    ck_d = dram_in("ck", (2 * PAST, D))
    cv_d = dram_in("cv", (2 * PAST, D))
    vec_d = dram_in("vec", (128, 226))
    cst_d = dram_in("cst", (128, 512))
    qpos_d = dram_in("qpos", (128, NJ * TP + 512))
    w_d = {n: dram_in(n, WSHAPES[n]) for n in WNAMES}

    yp_d = dram_out("y_p", (NJ * TP, D))
    ys_d = dram_out("y_s", (TS, D))
    oc_d = dram_out("o_conv", (6, D))
    kp_d = dram_out("k_p", (NJ * TP, D))
    vp_d = dram_out("v_p", (NJ * TP, D))
    ks_d = dram_out("k_s", (TS, D))
    vs_d = dram_out("v_s", (TS, D))

    wb_d = {n: nc.dram_tensor("b_" + n, list(WSHAPES[n]), BF16) for n in WNAMES}
    xres_d = nc.dram_tensor("xres", [(NJ + 1) * 128, KC * TP], F32)
    RKV = NJ * NH * 128
    qt_d = nc.dram_tensor("qt", [RKV, TP], BF16)
    RP = 512
    NPC = 8
    def _pieces(nm, rows):
        return [[nc.dram_tensor("%s_%d_%d" % (nm, i, q), [rows, TP], BF16) for q in range(NPC)] for i in range(NJ)]
    ktl_d = _pieces("ktl", RP)
    vl_d = _pieces("vl", RP)
    kt4_d = _pieces("kt4", 4 * RP)
    v4_d = _pieces("v4", 4 * RP)
    kt8_d = _pieces("kt8", 8 * RP)
    v8_d = _pieces("v8", 8 * RP)
    vsn_d = nc.dram_tensor("vsn", [TS, D], BF16)

    def dv(ap, name):
        return V(ap, [("dr", name)])

    uniq = [0]

    def dvu(ap):
        uniq[0] += 1
        return V(ap, [("dr", "u%d" % uniq[0])])

    A = Arena(nc, es, 53120)
    XT = A.alloc(KC * TP)
    HT = A.alloc(KC * TP // 2)
    GT = A.alloc(KC * TP // 2)
    WS = [A.alloc(4096) for _ in range(3)]
    S0 = A.alloc(2304)
    S1 = A.alloc(2048)
    N0 = A.alloc(1024)
    CST = A.alloc(512)
    CB = A.alloc(128, 8)
    VEC = A.alloc(226, 8)
    UH = A.alloc(KC * 8, 8)
    USV = A.alloc(KC * 6, 8)
    SCT = A.alloc(KC * 4, 8)
    QS = A.alloc(NH * NS // 2)
    KS = A.alloc(NH * NS // 2)
    SCR = A.alloc(8, 8)
    P.scratch = A.f32(SCR, 8).ap
    print("arena words used", A.top)

    def xt(kc, T, c0=0, c1=None):
        c1 = T if c1 is None else c1
        return A.f32(XT + kc * T + c0, c1 - c0)

    def ht(kc, T, c0=0, c1=None):
        c1 = T if c1 is None else c1
        return A.b16(HT, kc * T + c0, c1 - c0)

    def gt(kc, T, c0=0, c1=None):
        c1 = T if c1 is None else c1
        return A.b16(GT, kc * T + c0, c1 - c0)

    def ht_all(T):
        return A.b16(HT, 0, KC * T)

    def gt_all(T):
        return A.b16(GT, 0, KC * T)

    ident = A.f32(CST, 128)
    kpos = lambda b: A.f32(CST + 384 + b, 1)
    ltri = A.b16(CB, 0, 128)
    ones = A.b16(CB, 128, 128)
    vecv = lambda off, n=1: A.f32(VEC + off, n)
    G_MIX0, G_FFN0, G_MIX1, G_FFN1, WC0, WC1, WC2, GQ, GK = 0, 32, 64, 96, 128, 160, 192, 224, 225

    PS = []
    for i in range(8):
        t = es.enter_context(nc.psum_tensor("ps%d" % i, [128, 512], F32))
        PS.append(t)

    def ps(i, c0=0, c1=512, p0=0, p1=128):
        return V(PS[i][p0:p1, c0:c1], [("ps", i)])

    def mm(out, lhsT, rhs, start, stop):
        return lambda e: e.matmul(out.ap, lhsT.ap, rhs.ap, start=start, stop=stop)

    def act(out, in_, func, scale=1.0, bias=0.0):
        P.op("act", lambda e: e.activation(out.ap, in_.ap, func, bias=bias, scale=scale), R=[in_], W=[out])

    def acopy(eng, out, in_):
        if eng == "act":
            P.op("act", lambda e: e.activation(out.ap, in_.ap, AF.Copy), R=[in_], W=[out])
        else:
            P.op(eng, lambda e: e.tensor_copy(out.ap, in_.ap), R=[in_], W=[out])

    def tt(eng, out, a, b, op):
        P.op(eng, lambda e: e.tensor_tensor(out.ap, a.ap, b.ap, op), R=[a, b], W=[out])

    def ts(eng, out, a, s1, op0, s2=None, op1=None):
        R = [a] + [s for s in (s1, s2) if isinstance(s, V)]
        s1a = s1.ap if isinstance(s1, V) else s1
        s2a = s2.ap if isinstance(s2, V) else s2
        if op1 is None:
            P.op(eng, lambda e: e.tensor_scalar(out.ap, a.ap, s1a, None, op0), R=R, W=[out])
        else:
            P.op(eng, lambda e: e.tensor_scalar(out.ap, a.ap, s1a, s2a, op0, op1), R=R, W=[out])

    def stt(eng, out, a, s, b, op0, op1):
        R = [a, b] + ([s] if isinstance(s, V) else [])
        sa = s.ap if isinstance(s, V) else s
        P.op(eng, lambda e: e.scalar_tensor_tensor(out.ap, a.ap, sa, b.ap, op0, op1), R=R, W=[out])

    rr = [0]

    def evac_eng():
        rr[0] ^= 1
        return "act" if rr[0] else "dve"

    class WQ:
        def __init__(self):
            self.plan = []
            self.collect = True
            self.issued = 0
            self.taken = 0
            self.released = 0

        def release(self, n=1):
            if not self.collect:
                self.released += n

        def barrier(self):
            if self.collect:
                self.plan.append(("BARRIER",))
                return
            assert self.plan[self.taken] == ("BARRIER",), (self.taken, self.plan[self.taken])
            assert self.issued <= self.taken
            self.taken += 1
            self.issued = self.taken
            self.released += 1

        def _issue(self, i):
            name, r0, nr, c0, ncol = self.plan[i]
            slot = WS[i % 3]
            nkc = nr // 128
            dst = A.b16(slot, 0, nkc * ncol)
            src = wb_d[name][r0:r0 + nr, c0:c0 + ncol].rearrange("(k p) c -> p k c", p=128)
            blk = ("dr", "wb_%s_%d" % (name, wblock(name, r0, c0)))
            P.dma("sp", sub(dst, dst.ap.rearrange("p (k c) -> p k c", k=nkc)), V(src, [blk]))

        def get(self, name, r0, nr, c0, ncol):
            spec = (name, r0, nr, c0, ncol)
            if self.collect:
                self.plan.append(spec)
                return None
            i = self.taken
            assert self.plan[i] == spec, (i, self.plan[i], spec)
            while self.issued < min(len(self.plan), self.released + 3):
                if self.plan[self.issued] == ("BARRIER",):
                    break
                self._issue(self.issued)
                self.issued += 1
            assert self.issued > i, (self.issued, i, self.released)
            self.taken += 1
            nkc = nr // 128
            v = A.b16(WS[i % 3], 0, nkc * ncol)
            return sub(v, v.ap.rearrange("p (k c) -> p k c", k=nkc))

    wq = WQ()

    def wblock(name, r0, c0):
        if name.startswith("w_d"):
            return r0 // 512
        return c0 // 1024

    def convert_weights(names):
        for n in names:
            K, Fd = WSHAPES[n]
            if n.startswith("w_d"):
                for b in range((K + 511) // 512):
                    r0, r1 = b * 512, min(K, b * 512 + 512)
                    P.dma("pool", dv(wb_d[n][r0:r1, :], "wb_%s_%d" % (n, b)), dvu(w_d[n][r0:r1, :]),
                          semkey=("cv", b % 8), partial=True)
            else:
                for b in range((Fd + 1023) // 1024):
                    c0, c1 = b * 1024, min(Fd, b * 1024 + 1024)
                    for rb in range(K // 1024):
                        P.dma("pool", dv(wb_d[n][rb * 1024:(rb + 1) * 1024, c0:c1], "wb_%s_%d" % (n, b)),
                              dvu(w_d[n][rb * 1024:(rb + 1) * 1024, c0:c1]), semkey=("cv", (b + rb) % 8),
                              partial=True)

    def linear(name, col0, ncol, src_all, src_kc, T, consume, banks=(0, 1, 2, 3)):
        nm = ncol // 128
        outs = [ps(banks[m], 0, T) for m in range(nm)]
        for kh in range(2):
            wt = wq.get(name, kh * 2048, 2048, col0, ncol)
            if P.dry:
                continue
            for m in range(nm):
                fns = []
                for k in range(16):
                    kc = kh * 16 + k
                    fns.append(mm(outs[m], sub(wt, wt.ap[:, k, m * 128:(m + 1) * 128]), src_kc(kc),
                                  kc == 0, kc == KC - 1))

                def fn(e, fns=fns):
                    r = None
                    for f in fns:
                        r = f(e)
                    return r
                P.op("pe", fn, R=[wt, src_all], W=[outs[m]])
            wq.release(1)
        if P.dry:
            return
        for m in range(nm):
            consume(m, outs[m])

    def rmsnorm(T, goff):
        RT = A.f32(N0, T)
        RS = A.f32(N0 + 512, T)
        for kc in range(KC):
            act(ht(kc, T), xt(kc, T), AF.Square)
        out = ps(4, 0, T)

        def fn(e):
            r = None
            for kc in range(KC):
                r = e.matmul(out.ap, ones.ap, ht(kc, T).ap, start=(kc == 0), stop=(kc == KC - 1))
            return r
        P.op("pe", fn, R=[ones, ht_all(T)], W=[out])
        act(RT, out, AF.Sqrt, scale=1.0 / D, bias=EPS)
        P.op("dve", lambda e: e.reciprocal(RS.ap, RT.ap), R=[RT], W=[RS])
        for kc in range(KC):
            stt("dve", ht(kc, T), xt(kc, T), vecv(goff + kc), RS, ALU.mult, ALU.mult)

    def ffn(T, layer):
        wg, wu, wd = "w_g%d" % layer, "w_u%d" % layer, "w_d%d" % layer
        nfg = (DFF + 511) // 512
        for fg in range(nfg):
            c0 = fg * 512
            ncol = min(512, DFF - c0)
            nm = ncol // 128
            SG = lambda m: A.f32(S1 + m * T, T)
            par = fg % 2
            AT = lambda m: A.b16(S0, par * 4 * 512 + m * T, T)
            linear(wg, c0, ncol, ht_all(T), lambda kc: ht(kc, T), T,
                   lambda m, pv: act(SG(m), pv, AF.Silu))
            linear(wu, c0, ncol, ht_all(T), lambda kc: ht(kc, T), T,
                   lambda m, pv: tt("dve", AT(m), SG(m), pv, ALU.mult))
            for ch in range(2):
                wt = wq.get(wd, c0, ncol, ch * 2048, 2048)
                if P.dry:
                    continue
                atall = A.b16(S0, par * 4 * 512, nm * T)
                ats = [AT(k).ap for k in range(nm)]
                for mo in range(16):
                    b = mo % 4
                    out = ps(b, 0, T)

                    def fn(e, wt=wt, mo=mo, out=out, ats=ats):
                        r = None
                        for k in range(len(ats)):
                            r = e.matmul(out.ap, wt.ap[:, k, mo * 128:(mo + 1) * 128], ats[k],
                                         start=(k == 0), stop=(k == len(ats) - 1))
                        return r
                    P.op("pe", fn, R=[wt, atall], W=[out])
                    xc = xt(ch * 16 + mo, T)
                    tt("dve", xc, xc, out, ALU.add)
                wq.release(1)

    def load_xT(src_rows, ntok_list, T):
        for (dap, n, coff) in src_rows:
            for half in range(2):
                stg = A.f32(S0 if half == 0 else S1, 2048, 0, n)
                P.dma("sp", stg, dvu(dap[:, half * 2048:(half + 1) * 2048]))
                for q4 in range(4):
                    b = 4 + (q4 % 2)
                    pv = ps(b, 0, 4 * n)

                    def fn(e, stg=stg, q4=q4, n=n, pv=pv):
                        r = None
                        for i in range(4):
                            c = q4 * 4 + i
                            r = e.transpose(pv.ap[:, i * n:(i + 1) * n], stg.ap[:, c * 128:(c + 1) * 128],
                                            ident.ap[0:n, 0:n])
                        return r
                    P.op("pe", fn, R=[stg, ident], W=[pv])
                    kc0 = half * 16 + q4 * 4
                    dst = A.f32(XT + kc0 * T, 4 * T)
                    dap2 = dst.ap.rearrange("p (k t) -> p k t", k=4)[:, :, coff:coff + n]
                    sap = pv.ap.rearrange("p (k t) -> p k t", k=4)
                    eng = evac_eng()
                    acopy(eng, sub(dst, dap2), sub(pv, sap))

    def store_rows(src_fn, n, dst_ap):
        for half in range(2):
            stg = A.f32(S0 if half == 0 else S1, 2048, 0, n)
            for q4 in range(4):
                b = 4 + (q4 % 2)
                pv = ps(b, 0, 512, 0, n)
                srcs = [src_fn(half * 16 + q4 * 4 + i) for i in range(4)]
                pvw = ps(b, 0, 512)

                def fn(e, srcs=srcs, pvw=pvw):
                    r = None
                    for i in range(4):
                        r = e.transpose(pvw.ap[:, i * 128:(i + 1) * 128], srcs[i].ap, ident.ap)
                    return r
                P.op("pe", fn, R=srcs + [ident], W=[pvw])
                acopy(evac_eng(), sub(stg, stg.ap[:, q4 * 512:(q4 + 1) * 512]), pv)
            P.dma("pool", dvu(dst_ap[:, half * 2048:(half + 1) * 2048]), stg)

    class _Stop(Exception):
        pass

    def chk(name):
        if STOP == name:
            if not P.dry:
                Td = TS
                P.dma("pool", dvu(yp_d[0:128, :]), A.f32(XT, KC * Td))
                P.dma("pool", dvu(yp_d[128:256, :]), A.b16(HT, 0, KC * Td))
                P.dma("pool", dvu(yp_d[256:384, :]), A.b16(GT, 0, KC * Td))
            raise _Stop()

    def run_all():
        try:
            run_all_()
        except _Stop:
            pass

    def run_all_():
        P.dma("sp", A.f32(CST, 512), dvu(cst_d[:, :]))
        P.dma("sp", A.f32(VEC, 226), dvu(vec_d[:, :]))
        acopy("dve", A.b16(CB, 0, 256), A.f32(CST + 128, 256))
        convert_weights(["w_in", "w_cout", "w_g0", "w_u0", "w_d0", "w_qkv", "w_o", "w_g1", "w_u1", "w_d1"])
        stg = A.f32(S0, D, 0, 4)
        P.dma("sp", stg, dvu(sc_d[:, :]))
        for q8 in range(8):
            pv = ps(4 + q8 % 2, 0, 16)

            def fn(e, q8=q8, pv=pv, stg=stg):
                r = None
                for i in range(4):
                    c = q8 * 4 + i
                    r = e.transpose(pv.ap[:, i * 4:(i + 1) * 4], stg.ap[:, c * 128:(c + 1) * 128], ident.ap[0:4, 0:4])
                return r
            P.op("pe", fn, R=[stg, ident], W=[pv])
            acopy("dve", A.f32(SCT + q8 * 16, 16), pv)

        chk("init")
        g4 = [[0, 1, 2, 3], [4, 5, 6, 7]]
        g2 = [[0, 4], [1, 5], [2, 6], [3, 7]]
        for t in range(NJ + 1):
            T = TS if t == 0 else TP
            j = t - 1
            if t == 0:
                load_xT([(xs_d[:, :], TS, 0)], None, T)
            else:
                load_xT([(xp_d[j * TP + b * 128: j * TP + (b + 1) * 128, :], 128, b * 128) for b in range(4)], None, T)
            chk("load%d" % t)
            rmsnorm(T, G_MIX0)
            chk("norm%d" % t)
            for cg in range(8):
                CSB = lambda m: A.f32(S1 + m * T, T)
                UW = T + 4 if t == 0 else T + 2
                UE = lambda m, c0, n: A.f32(S0 + m * UW + c0, n)
                linear("w_in", D + cg * 512, 512, ht_all(T), lambda kc: ht(kc, T), T,
                       lambda m, pv: acopy("act", CSB(m), pv))
                if not P.dry:
                    if t == 0:
                        for s in range(2):
                            dst = A.f32(S0, 4 * UW)
                            d2 = dst.ap.rearrange("p (m u) -> p m u", m=4)[:, :, s * 18:s * 18 + 2]
                            src = A.f32(SCT + cg * 16, 16)
                            s2 = src.ap.rearrange("p (m u) -> p m u", m=4)[:, :, s * 2:s * 2 + 2]
                            acopy("dve", sub(dst, d2), sub(src, s2))
                    else:
                        dst = A.f32(S0, 4 * UW)
                        d2 = dst.ap.rearrange("p (m u) -> p m u", m=4)[:, :, 0:2]
                        src = A.f32(UH + cg * 32, 32)
                        s2 = src.ap.rearrange("p (m u) -> p m u", m=4)[:, :, 2 * j:2 * j + 2]
                        acopy("dve", sub(dst, d2), sub(src, s2))

                def cons_x(m, pv, cg=cg, CSB=CSB, UE=UE, T=T, t=t):
                    kc = cg * 4 + m
                    if t == 0:
                        for s in range(2):
                            tt("dve", UE(m, s * 18 + 2, 16), sub(CSB(m), CSB(m).ap[:, s * 16:s * 16 + 16]),
                               sub(pv, pv.ap[:, s * 16:s * 16 + 16]), ALU.mult)
                            acopy("dve", A.f32(USV + kc * 6 + 2 * s, 2), UE(m, s * 18 + 16, 2))
                        tt("dve", A.f32(UH + kc * 8, 8), sub(CSB(m), CSB(m).ap[:, 32:40]),
                           sub(pv, pv.ap[:, 32:40]), ALU.mult)
                    else:
                        tt("dve", UE(m, 2, T), CSB(m), pv, ALU.mult)
                        if t == NJ:
                            acopy("dve", A.f32(USV + kc * 6 + 4, 2), UE(m, T, 2))
                    segs = [(0, 0, 16), (18, 16, 16)] if t == 0 else [(0, 0, T)]
                    for (u0, c0, n) in segs:
                        cvv = sub(CSB(m), CSB(m).ap[:, c0:c0 + n])
                        ts("dve", cvv, UE(m, u0 + 2, n), vecv(WC2 + kc), ALU.mult)
                        stt("dve", cvv, UE(m, u0 + 1, n), vecv(WC1 + kc), cvv, ALU.mult, ALU.add)
                        stt("dve", cvv, UE(m, u0, n), vecv(WC0 + kc), cvv, ALU.mult, ALU.add)
                    if t == 0:
                        P.op("dve", lambda e, v=CSB(m): e.memset(v.ap[:, 32:TS], 0.0), W=[CSB(m)])
                linear("w_in", 2 * D + cg * 512, 512, ht_all(T), lambda kc: ht(kc, T), T, cons_x)
                linear("w_in", cg * 512, 512, ht_all(T), lambda kc: ht(kc, T), T,
                       lambda m, pv, cg=cg, CSB=CSB: tt("dve", gt(cg * 4 + m, T), CSB(m), pv, ALU.mult))
            for cg in range(8):
                def cons_o(m, pv, cg=cg, T=T):
                    xc = xt(cg * 4 + m, T)
                    tt("dve", xc, xc, pv, ALU.add)
                linear("w_cout", cg * 512, 512, gt_all(T), lambda kc: gt(kc, T), T, cons_o)
            chk("conv%d" % t)
            rmsnorm(T, G_FFN0)
            ffn(T, 0)
            chk("ffn%d" % t)
            rmsnorm(T, G_MIX1)
            TV = NS if t == 0 else TP
            for which in range(2):
                for cg in range(8):
                    KSTG = A.f32(S0, 4 * 512)

                    def cons_qk(m, pv, which=which, cg=cg, T=T, t=t, j=j, TV=TV, KSTG=KSTG):
                        h = cg * 4 + m
                        QF = A.f32(S1, T)
                        SQ = A.b16(S1 + 512, 0, T)
                        RT = A.f32(S1 + 768, T)
                        RS = A.f32(S1 + 1280, T)
                        QN = A.b16(S1 + 1792, 0, T)
                        if os.environ.get("K_QMODE", "0") == "1":
                            acopy("dve", QF, pv)
                            return
                        acopy("act", QF, pv)
                        act(SQ, QF, AF.Square)
                        ss = ps(5, 0, T)
                        if os.environ.get("K_QMODE", "0") == "3":
                            P.op("dve", lambda e: e.memset(RS.ap, 1.0), W=[RS])
                        else:
                            P.op("pe", mm(ss, ones, SQ, True, True), R=[ones, SQ], W=[ss])
                            act(RT, ss, AF.Sqrt, scale=1.0 / HD, bias=EPS)
                            P.op("dve", lambda e: e.reciprocal(RS.ap, RT.ap), R=[RT], W=[RS])
                        gv = vecv(GQ if which == 0 else GK)
                        if which == 1:
                            stt("dve", QF, QF, gv, RS, ALU.mult, ALU.mult)
                            acopy("act", QN, QF)
                        else:
                            stt("dve", QN, QF, gv, RS, ALU.mult, ALU.mult)
                        if t == 0:
                            dstb = A.b16(QS if which == 0 else KS, h * NS, NS)
                            if os.environ.get("K_QMODE", "0") != "2":
                                acopy("dve", dstb, sub(QN, QN.ap[:, 0:NS]))
                        else:
                            if which == 0:
                                row = (j * NH + h) * 128
                                P.dma("pool", dv(qt_d[row:row + 128, :], "qt%d" % j), QN, partial=True)
                            else:
                                P.dma("pool", dv(ktl_d[j][h // 4][(h % 4) * 128:(h % 4 + 1) * 128, :], "ktl%d_%d" % (j, h // 4)), QN,
                                      partial=True)
                        if which == 1:
                            nb = 1 if t == 0 else 4
                            n = 128
                            pv2 = ps(6, 0, 512, 0, n)

                            QFW = A.f32(S1, 512)
                            pvw = ps(6, 0, 512)

                            def fn(e, QFW=QFW, nb=nb, pvw=pvw):
                                r = None
                                for tb in range(nb):
                                    r = e.transpose(pvw.ap[:, tb * 128:(tb + 1) * 128], QFW.ap[:, tb * 128:(tb + 1) * 128],
                                                    ident.ap)
                                return r
                            P.op("pe", fn, R=[QFW, ident], W=[pvw])
                            d3 = KSTG.ap[0:n, :].rearrange("p (b c) -> p b c", b=4)[:, 0:nb, m * 128:(m + 1) * 128]
                            s3 = pv2.ap[:, 0:nb * 128].rearrange("p (b c) -> p b c", b=nb)
                            acopy("act", sub(KSTG, d3), sub(pv2, s3))
                            if m == 3:
                                if t == 0:
                                    P.dma("pool", dvu(ks_d[:, cg * 512:(cg + 1) * 512]),
                                          sub(KSTG, KSTG.ap[:, 0:512]))
                                else:
                                    dst = kp_d[j * TP:(j + 1) * TP, cg * 512:(cg + 1) * 512].rearrange(
                                        "(b p) c -> p b c", p=128)
                                    P.dma("pool", dvu(dst), sub(KSTG, KSTG.ap.rearrange("p (b c) -> p b c", b=4)))
                    linear("w_qkv", which * D + cg * 512, 512, ht_all(T), lambda kc: ht(kc, T), T, cons_qk)
                chk("qk%d_%d" % (which, t))
            for cg in range(8):
                nb = 1 if t == 0 else 4
                n = 128
                VSTG = A.f32(S0, 4 * 512)
                VB = A.b16(S1, 0, 4 * 512)
                outs = [ps(tb, 0, 512, 0, n) for tb in range(nb)]
                for kh in range(2):
                    wt = wq.get("w_qkv", kh * 2048, 2048, 2 * D + cg * 512, 512)
                    if P.dry:
                        continue
                    for tb in range(nb):
                        def fn(e, wt=wt, tb=tb, n=n, out=outs[tb], T=T, kh=kh):
                            r = None
                            for k in range(16):
                                kc = kh * 16 + k
                                r = e.matmul(out.ap, ht(kc, T).ap[:, tb * 128:tb * 128 + n], wt.ap[:, k, :],
                                             start=(kc == 0), stop=(kc == KC - 1))
                            return r
                        P.op("pe", fn, R=[wt, ht_all(T)], W=[outs[tb]])
                    wq.release(1)
                if P.dry:
                    continue
                for tb in range(nb):
                    out = outs[tb]
                    acopy("act", sub(VSTG, VSTG.ap[0:n, tb * 512:(tb + 1) * 512]), out)
                    acopy("dve", sub(VB, VB.ap[0:n, tb * 512:(tb + 1) * 512]),
                          sub(VSTG, VSTG.ap[0:n, tb * 512:(tb + 1) * 512]))
                if t == 0:
                    P.dma("pool", dvu(vs_d[:, cg * 512:(cg + 1) * 512]), sub(VSTG, VSTG.ap[:, 0:512]))
                    P.dma("pool", dv(vsn_d[:, cg * 512:(cg + 1) * 512], "vsn"), sub(VB, VB.ap[:, 0:512]), partial=True)
                else:
                    dst = vp_d[j * TP:(j + 1) * TP, cg * 512:(cg + 1) * 512].rearrange("(b p) c -> p b c", p=128)
                    P.dma("pool", dvu(dst), sub(VSTG, VSTG.ap.rearrange("p (b c) -> p b c", b=4)))
                    for m in range(4):
                        h = cg * 4 + m
                        dst = vl_d[j][h // 4][(h % 4) * 128:(h % 4 + 1) * 128, :].rearrange("p (b c) -> p b c", b=4)
                        src = VB.ap.rearrange("p (b c) -> p b c", b=4)[:, :, m * 128:(m + 1) * 128]
                        P.dma("pool", dv(dst, "vl%d_%d" % (j, h // 4)), sub(VB, src), partial=True)
            chk("qkv%d" % t)
            xa = A.f32(XT, KC * T)
            P.dma("pool", dv(xres_d[t * 128:(t + 1) * 128, 0:KC * T], "xres%d" % t), xa)
            if t >= 2:
                for q in range(NPC):
                    jp = j - 1
                    P.coll_issue(g2, dv(kt4_d[jp][q].ap().opt(), "kt4_%d_%d" % (jp, q)),
                                 dv(kt8_d[jp][q].ap().opt(), "kt8_%d_%d" % (jp, q)))
                    P.coll_issue(g2, dv(v4_d[jp][q].ap().opt(), "v4_%d_%d" % (jp, q)),
                                 dv(v8_d[jp][q].ap().opt(), "v8_%d_%d" % (jp, q)))
            if t >= 1:
                for q in range(NPC):
                    P.coll_issue(g4, dv(ktl_d[j][q].ap().opt(), "ktl%d_%d" % (j, q)),
                                 dv(kt4_d[j][q].ap().opt(), "kt4_%d_%d" % (j, q)))
                    P.coll_issue(g4, dv(vl_d[j][q].ap().opt(), "vl%d_%d" % (j, q)),
                                 dv(v4_d[j][q].ap().opt(), "v4_%d_%d" % (j, q)))
            if t == NJ:
                stg = A.f32(S0, D, 0, 6)
                for q8 in range(8):
                    pv = ps(4 + q8 % 2, 0, 512, 0, 6)
                    pvw = ps(4 + q8 % 2, 0, 512)
                    usrc = [A.f32(USV + (q8 * 4 + i) * 6, 128) for i in range(4)]

                    def fn(e, usrc=usrc, pvw=pvw):
                        r = None
                        for i in range(4):
                            r = e.transpose(pvw.ap[:, i * 128:(i + 1) * 128], usrc[i].ap, ident.ap)
                        return r
                    P.op("pe", fn, R=usrc + [ident], W=[pvw])
                    acopy("dve", sub(stg, stg.ap[:, q8 * 512:(q8 + 1) * 512]), pv)
                P.dma("pool", dvu(oc_d[:, :]), stg)

        chk("A")
        P.coll_drain()
        jl = NJ - 1
        for q in range(NPC):
            P.coll_issue(g2, dv(kt4_d[jl][q].ap().opt(), "kt4_%d_%d" % (jl, q)),
                         dv(kt8_d[jl][q].ap().opt(), "kt8_%d_%d" % (jl, q)))
            P.coll_issue(g2, dv(v4_d[jl][q].ap().opt(), "v4_%d_%d" % (jl, q)),
                         dv(v8_d[jl][q].ap().opt(), "v8_%d_%d" % (jl, q)))
        P.coll_drain()

        chk("coll")
        def attn_unit(Pk, zfn, Ncols, mask_kpos, QP, bufs, first, last, vfn, Obank, zb, sb_):
            E, SPv, Wt, Av, CUM, CUMb = bufs
            Zv = ps(zb, 0, Ncols, 0, Pk)
            zfn(Zv)
            Ev = sub(E, E.ap[0:Pk, 0:Ncols])
            act(Ev, Zv, AF.Exp, scale=SB_SCALE)
            if mask_kpos is not None:
                stt("dve", Ev, sub(QP, QP.ap[0:Pk, 0:Ncols]), sub(mask_kpos, mask_kpos.ap[0:Pk, :]), Ev,
                    ALU.is_gt, ALU.mult)
            SPp = sub(SPv, SPv.ap[0:Pk, 0:Ncols])
            act(SPp, Ev, AF.Ln, bias=1.0)
            Sv = ps(sb_, 0, Ncols, 0, Pk)
            lt = sub(ltri, ltri.ap[0:Pk, 0:Pk])
            if first:
                P.op("pe", mm(Sv, lt, SPp, True, True), R=[ltri, SPp], W=[Sv])
            else:
                def fn(e):
                    e.matmul(Sv.ap, lt.ap, SPp.ap, start=True, stop=False)
                    return e.matmul(Sv.ap, ones.ap[:, 0:Pk], CUMb.ap[:, 0:Ncols], start=False, stop=True)
                P.op("pe", fn, R=[ltri, ones, SPp, CUMb], W=[Sv])
            Wp = sub(Wt, Wt.ap[0:Pk, 0:Ncols])
            act(Wp, Sv, AF.Exp, scale=-1.0)
            Ap = sub(Av, Av.ap[0:Pk, 0:Ncols])
            tt("dve", Ap, Ev, Wp, ALU.mult)
            vfn(Ap, first, last)
            if not last:
                Cp = sub(CUM, CUM.ap[0:Pk, 0:Ncols])
                if first:
                    if Pk < 128:
                        P.op("pool", lambda e: e.memset(CUM.ap[:, 0:Ncols], 0.0), W=[CUM])
                    acopy("pool", Cp, SPp)
                else:
                    tt("pool", Cp, Cp, SPp, ALU.add)
                acopy("pool", sub(CUMb, CUMb.ap[:, 0:Ncols]), sub(CUM, CUM.ap[:, 0:Ncols]))

        for t in list(range(1, NJ + 1)) + [0]:
            T = TS if t == 0 else TP
            TV = NS if t == 0 else TP
            j = t - 1
            wq.barrier()
            if t != 0:
                base = WS[0]
                QP = A.f32(base, 512)
                P.dma("sp", QP, dvu(qpos_d[:, j * TP:(j + 1) * TP]))
                sets = []
                for i in range(2):
                    o = base + 512 + i * 2304
                    sets.append((A.f32(o, 512), A.b16(o + 1536, 0, 512), A.f32(o + 512, 512),
                                 A.b16(o + 1536, 512, 512), A.f32(o + 1024, 512), A.b16(o + 2048, 0, 512)))
                tb0 = base + 512 + 2 * 2304
                QTt = [A.b16(tb0 + i * 256, 0, 512) for i in range(2)]
                KTt = [A.b16(tb0 + 512 + i * 256, 0, 512) for i in range(3)]
                VTt = [A.b16(tb0 + 1280 + i * 256, 0, 512) for i in range(3)]
                nkt = 8 * j + 8
                li = 0
                for h in range(NH):
                    row = (j * NH + h) * 128
                    Qv = QTt[h % 2]
                    P.dma("sp", Qv, dv(qt_d[row:row + 128, :], "qt%d" % j))
                    Ob = 6 + (h % 2)
                    Ov = ps(Ob, 0, 512)
                    bufs = sets[h % 2]
                    nunits = nkt * 4
                    ui = 0
                    for kt in range(nkt - 1, -1, -1):
                        cc, jj = kt % 8, kt // 8
                        r0 = cc * RP + (h % 4) * 128
                        Kv = KTt[li % 3]
                        Vv = VTt[li % 3]
                        li += 1
                        P.dma("sp", Kv, dv(kt8_d[jj][h // 4][r0:r0 + 128, :], "kt8_%d_%d" % (jj, h // 4)))
                        P.dma("sp", Vv, dv(v8_d[jj][h // 4][r0:r0 + 128, :], "v8_%d_%d" % (jj, h // 4)))
                        for kb in range(3, -1, -1):
                            def zfn(Zv, Kv=Kv, kb=kb, Qv=Qv):
                                P.op("pe", mm(Zv, sub(Kv, Kv.ap[:, kb * 128:(kb + 1) * 128]), Qv, True, True),
                                     R=[Kv, Qv], W=[Zv])

                            def vfn(Ap, first, last, Vv=Vv, kb=kb, Ov=Ov):
                                P.op("pe", mm(Ov, sub(Vv, Vv.ap[:, kb * 128:(kb + 1) * 128]), Ap, first, last),
                                     R=[Vv, Ap], W=[Ov])
                            mk = kpos(4 * kt + kb) if kt >= 8 * j else None
                            attn_unit(128, zfn, 512, mk, QP, bufs, ui == 0, ui == nunits - 1, vfn, Ob,
                                      4 + (ui % 2), 2 + (ui % 2))
                            ui += 1
                    acopy("act" if h % 2 else "dve", gt(h, T), Ov)
            else:
                QP = A.f32(S1, 512)
                P.dma("sp", QP, dvu(qpos_d[:, NJ * TP:NJ * TP + 512]))
                o = S0
                bufs = (A.f32(o, 512), A.b16(o + 1536, 0, 512), A.f32(o + 512, 512),
                        A.b16(o + 1536, 512, 512), A.f32(o + 1024, 512), A.b16(o + 2048, 0, 512))
                KST = A.f32(WS[0], D)
                VST = A.f32(WS[1], D)
                KTC = A.b16(WS[2], 0, D)
                VC = A.b16(WS[2] + 2048, 0, D)
                for s in range(2):
                    Ov = ps(6 + s, 0, 512)
                    if not P.dry:
                        P.op("pool", lambda e: e.memset(KTC.ap, 0.0), W=[KTC])
                        P.op("pool", lambda e: e.memset(VC.ap, 0.0), W=[VC])
                        P.op("pe", lambda e, Ov=Ov: e.matmul(Ov.ap, KTC.ap[:, 0:128], KTC.ap[:, 0:512],
                                                             start=True, stop=False), R=[KTC], W=[Ov])
                        ksall = A.b16(KS, 0, NH * NS)
                        kd = KTC.ap.rearrange("p (h k) -> p h k", h=NH)[:, :, 0:16]
                        ksrc = ksall.ap.rearrange("p (h k) -> p h k", h=NH)[:, :, s * 16:(s + 1) * 16]
                        acopy("dve", sub(KTC, kd), sub(ksall, ksrc))
                        P.dma("sp", sub(VC, VC.ap[0:16, :]), dv(vsn_d[s * 16:(s + 1) * 16, :], "vsn"))

                    def zfn0(Zv, s=s):
                        def fn(e):
                            r = None
                            for h in range(NH):
                                qv = A.b16(QS, h * NS + s * 16, 16)
                                r = e.matmul(Zv.ap[:, h * 16:(h + 1) * 16], KTC.ap[:, h * 128:(h + 1) * 128], qv.ap,
                                             start=True, stop=True)
                            return r
                        P.op("pe", fn, R=[KTC, A.b16(QS, 0, NH * NS)], W=[Zv])

                    def vfn0(Ap, first, last, Ov=Ov):
                        def fn(e):
                            r = None
                            for h in range(NH):
                                r = e.matmul(Ov.ap[:, h * 16:(h + 1) * 16], VC.ap[:, h * 128:(h + 1) * 128],
                                             Ap.ap[:, h * 16:(h + 1) * 16], start=False, stop=False)
                            return r
                        P.op("pe", fn, R=[VC, Ap], W=[Ov])
                    attn_unit(128, zfn0, 512, kpos(PAST // 128), QP, bufs, True, False, vfn0, 6 + s, 4, 2)
                    for kb in range(PAST // 128 - 1, -1, -1):
                        r0 = s * PAST + kb * 128
                        P.dma("sp", KST, dvu(ck_d[r0:r0 + 128, :]))
                        P.dma("sp", VST, dvu(cv_d[r0:r0 + 128, :]))
                        for q8 in range(8):
                            pv = ps(q8 % 2, 0, 512)

                            def fn(e, q8=q8, pv=pv):
                                r = None
                                for i in range(4):
                                    c = q8 * 4 + i
                                    r = e.transpose(pv.ap[:, i * 128:(i + 1) * 128], KST.ap[:, c * 128:(c + 1) * 128],
                                                    ident.ap)
                                return r
                            P.op("pe", fn, R=[KST, ident], W=[pv])
                            acopy(evac_eng(), sub(KTC, KTC.ap[:, q8 * 512:(q8 + 1) * 512]), pv)
                        acopy("pool", VC, VST)

                        def zfn2(Zv, s=s):
                            def fn(e):
                                r = None
                                for h in range(NH):
                                    qv = A.b16(QS, h * NS + s * 16, 16)
                                    r = e.matmul(Zv.ap[:, h * 16:(h + 1) * 16], KTC.ap[:, h * 128:(h + 1) * 128], qv.ap,
                                                 start=True, stop=True)
                                return r
                            P.op("pe", fn, R=[KTC, A.b16(QS, 0, NH * NS)], W=[Zv])

                        def vfn2(Ap, first, last, Ov=Ov, kb=kb):
                            def fn(e):
                                r = None
                                for h in range(NH):
                                    r = e.matmul(Ov.ap[:, h * 16:(h + 1) * 16], VC.ap[:, h * 128:(h + 1) * 128],
                                                 Ap.ap[:, h * 16:(h + 1) * 16], start=False, stop=(kb == 0))
                                return r
                            P.op("pe", fn, R=[VC, Ap], W=[Ov])
                        attn_unit(128, zfn2, 512, None, QP, bufs, False, kb == 0, vfn2, 6 + s, 4 + (kb % 2), 2 + (kb % 2))
                    gall = A.b16(GT, 0, KC * T)
                    d2 = gall.ap.rearrange("p (h t) -> p h t", h=NH)[:, :, s * 16:(s + 1) * 16]
                    s2 = Ov.ap.rearrange("p (h q) -> p h q", h=NH)
                    acopy("dve", sub(gall, d2), sub(Ov, s2))
                if not P.dry:
                    P.op("pool", lambda e, T=T: e.memset(gt_all(T).ap.rearrange("p (h t) -> p h t", h=NH)[:, :, NS:T], 0.0),
                         W=[gt_all(T)])
            chk("attn%d" % t)
            xa = A.f32(XT, KC * T)
            P.dma("sp", xa, dv(xres_d[t * 128:(t + 1) * 128, 0:KC * T], "xres%d" % t))
            for cg in range(8):
                def cons_o(m, pv, cg=cg, T=T):
                    xc = xt(cg * 4 + m, T)
                    tt("dve", xc, xc, pv, ALU.add)
                linear("w_o", cg * 512, 512, gt_all(T), lambda kc: gt(kc, T), T, cons_o)
            rmsnorm(T, G_FFN1)
            ffn(T, 1)
            if t == 0:
                store_rows(lambda kc, T=T: xt(kc, T), 128, ys_d[:, :])
            else:
                for tb in range(4):
                    store_rows(lambda kc, tb=tb: sub(xt(kc, T), xt(kc, T).ap[:, tb * 128:(tb + 1) * 128]), 128,
                               yp_d[j * TP + tb * 128: j * TP + (tb + 1) * 128, :])

    P.dry = True
    run_all()
    P.dry = False
    wq.collect = False
    run_all()
    assert STOP or wq.taken == len(wq.plan)
    P.emit()
    es.close()
    return nc


_NC_CACHE = {}


def _fm(v):
    return np.ascontiguousarray(np.asarray(v, np.float32).reshape(KC, 128).T)


def kernel(x_prompt, x_sample, state_conv, cache_k, cache_v, g_mix, g_ffn,
           w_conv_in, w_conv, w_conv_out, w_qkv, g_q, g_k, w_o, w_gate, w_up, w_down):
    f = lambda a: np.asarray(a, dtype=np.float32)
    x_prompt, x_sample, state_conv = f(x_prompt), f(x_sample), f(state_conv)
    cache_k, cache_v = f(cache_k), f(cache_v)
    g_mix, g_ffn, w_conv = f(g_mix), f(g_ffn), f(w_conv)
    g_q, g_k = f(g_q), f(g_k)
    if "nc" not in _NC_CACHE:
        _NC_CACHE["nc"] = build_program()
    nc = _NC_CACHE["nc"]

    vec = np.zeros((128, 226), np.float32)
    vec[:, 0:32] = _fm(g_mix[0]); vec[:, 32:64] = _fm(g_ffn[0])
    vec[:, 64:96] = _fm(g_mix[1]); vec[:, 96:128] = _fm(g_ffn[1])
    vec[:, 128:160] = _fm(w_conv[0, 0]); vec[:, 160:192] = _fm(w_conv[0, 1]); vec[:, 192:224] = _fm(w_conv[0, 2])
    vec[:, 224] = g_q[0]; vec[:, 225] = g_k[0]
    cst = np.zeros((128, 512), np.float32)
    idx = np.arange(128)
    cst[:, 0:128] = np.eye(128, dtype=np.float32)
    cst[:, 128:256] = (idx[:, None] >= idx[None, :]).astype(np.float32)
    cst[:, 256:384] = 1.0
    cst[:, 384:512] = (128 * idx[None, :] + idx[:, None]).astype(np.float32)
    weights = {"w_in": f(w_conv_in)[0], "w_cout": f(w_conv_out)[0], "w_qkv": f(w_qkv)[0], "w_o": f(w_o)[0],
               "w_g0": f(w_gate)[0], "w_u0": f(w_up)[0], "w_d0": f(w_down)[0],
               "w_g1": f(w_gate)[1], "w_u1": f(w_up)[1], "w_d1": f(w_down)[1]}
    xp = x_prompt[0]
    in_maps = []
    for c in range(NCORES):
        xs = np.zeros((TS, D), np.float32)
        xs[0:16] = x_sample[2 * c]; xs[16:32] = x_sample[2 * c + 1]
        qpos = np.zeros((128, NJ * TP + 512), np.float32)
        xpc = np.empty((NJ * TP, D), np.float32)
        for j in range(NJ):
            g = 8 * j + c
            xpc[j * TP:(j + 1) * TP] = xp[g * TP:(g + 1) * TP]
            if g > 0:
                xs[32 + 2 * j: 34 + 2 * j] = xp[g * TP - 2: g * TP]
            qpos[:, j * TP:(j + 1) * TP] = (g * TP + np.arange(TP, dtype=np.float32))[None, :]
        qpos[:, NJ * TP:] = np.tile(PAST + np.arange(16, dtype=np.float32), NH)[None, :]
        m = {"xs": xs, "xp": xpc, "sc": np.ascontiguousarray(state_conv[0, 2 * c:2 * c + 2].reshape(4, D)),
             "ck": np.ascontiguousarray(cache_k[0, 2 * c:2 * c + 2].reshape(2 * PAST, D)),
             "cv": np.ascontiguousarray(cache_v[0, 2 * c:2 * c + 2].reshape(2 * PAST, D)),
             "vec": vec, "cst": cst, "qpos": qpos}
        m.update(weights)
        in_maps.append(m)
    res = run_bass_kernel_spmd(nc, in_maps, core_ids=list(range(NCORES))).results

    y_p = np.zeros((1, SEQ, D), np.float32)
    k_p = np.zeros((1, 1, SEQ, NH, HD), np.float32)
    v_p = np.zeros((1, 1, SEQ, NH, HD), np.float32)
    y_s = np.empty((16, 16, D), np.float32)
    k_s = np.empty((1, 16, 16, NH, HD), np.float32)
    v_s = np.empty((1, 16, 16, NH, HD), np.float32)
    c_s = np.empty((1, 16, 2, D), np.float32)
    c_p = np.empty((1, 1, 2, D), np.float32)
    for c in range(NCORES):
        r = res[c]
        for j in range(NJ):
            g = 8 * j + c
            y_p[0, g * TP:(g + 1) * TP] = r["y_p"][j * TP:(j + 1) * TP]
            k_p[0, 0, g * TP:(g + 1) * TP] = np.asarray(r["k_p"][j * TP:(j + 1) * TP]).reshape(TP, NH, HD)
            v_p[0, 0, g * TP:(g + 1) * TP] = np.asarray(r["v_p"][j * TP:(j + 1) * TP]).reshape(TP, NH, HD)
        y_s[2 * c:2 * c + 2] = np.asarray(r["y_s"])[0:NS].reshape(2, 16, D)
        k_s[0, 2 * c:2 * c + 2] = np.asarray(r["k_s"])[0:NS].reshape(2, 16, NH, HD)
        v_s[0, 2 * c:2 * c + 2] = np.asarray(r["v_s"])[0:NS].reshape(2, 16, NH, HD)
        oc = np.asarray(r["o_conv"])
        c_s[0, 2 * c:2 * c + 2] = oc[0:4].reshape(2, 2, D)
        if c == NCORES - 1:
            c_p[0, 0] = oc[4:6]
    return (y_p, y_s, c_p, c_s, k_p, v_p, k_s, v_s)
```
